# Optimizing a Trainium2 kernel written in Bass

```python
import jax, jax.numpy as jnp
from jax import lax
import numpy as np

D_MODEL = 2048
BATCH = 4
SEQ = 4096
DEPTH = 4

N_META = 16
N_MIXERS = 2
N_RWKV = (DEPTH + 1) // 2
N_GLA = DEPTH // 2
D_FF = 4 * D_MODEL
NORM_EPS = 1e-6

RW_HEAD = 64
RW_HEADS = D_MODEL // RW_HEAD
RW_DECAY_LORA = 96
RW_AAA_LORA = 96
RW_MV_LORA = 64
RW_GATE_LORA = 256
RW_GN_EPS = 64e-5

GLA_HEADS = 4
GLA_DK = D_MODEL // 2
GLA_DV = D_MODEL
GLA_HK = GLA_DK // GLA_HEADS
GLA_HV = GLA_DV // GLA_HEADS
GLA_GATE_LORA = 16
GLA_TAU = 16.0
GLA_CHUNK = 64
GLA_HEAD_EPS = 1e-5
GLA_IN = 2 * GLA_DK + 2 * GLA_DV + GLA_GATE_LORA

kernel_name = 'meta_rwkv7_gla_sqrelu_hybrid'


def rms_norm(x, g, eps=NORM_EPS):
    xf = x.astype(jnp.float32)
    y = xf * lax.rsqrt(jnp.mean(xf * xf, axis=-1, keepdims=True) + eps)
    return (y * g.astype(jnp.float32)).astype(x.dtype)


def sq_relu_mlp(x, w1, w2):
    h = jax.nn.relu(x @ w1)
    return (h * h) @ w2


def rwkv7_scan(r, w, k, v, a, b):
    B, T, H, N = r.shape

    def step(S, inp):
        r_t, w_t, k_t, v_t, a_t, b_t = inp
        sa = jnp.einsum('bhvk,bhk->bhv', S, a_t)
        S = S * w_t[:, :, None, :] + sa[..., None] * b_t[:, :, None, :] + v_t[..., None] * k_t[:, :, None, :]
        y = jnp.einsum('bhvk,bhk->bhv', S, r_t)
        return S, y

    xs = tuple(jnp.moveaxis(t, 1, 0) for t in (r, w, k, v, a, b))
    S0 = jnp.zeros((B, H, N, N), jnp.float32)
    _, y = lax.scan(step, S0, xs)
    return jnp.moveaxis(y, 0, 1)


def rwkv7_time_mix(x, v_first, mix, w_rkv, w0, w1, w2, a0, a1, a2, g1, g2, k_k, k_a, r_k,
                   ln_w, ln_b, w_o, vres):
    B, T, D = x.shape
    xx = jnp.pad(x, ((0, 0), (1, 0), (0, 0)))[:, :-1] - x
    xr, xw, xk, xv, xa, xg = (x + xx * mix[i] for i in range(6))
    r = xr @ w_rkv[0]
    k = xk @ w_rkv[1]
    v = xv @ w_rkv[2]
    if vres is None:
        v_first = v
    else:
        v0, v1, v2 = vres
        v = v + (v_first - v) * jax.nn.sigmoid(v0 + (xv @ v1) @ v2)
    w_log = -jax.nn.softplus(-(w0 + jnp.tanh(xw @ w1) @ w2)) - 0.5
    a = jax.nn.sigmoid(a0 + (xa @ a1) @ a2)
    g = jax.nn.sigmoid(xg @ g1) @ g2
    hs = lambda t: t.reshape(B, T, RW_HEADS, RW_HEAD).astype(jnp.float32)
    kk = hs(k * k_k)
    kk = kk / jnp.maximum(jnp.sqrt(jnp.sum(kk * kk, axis=-1, keepdims=True)), 1e-12)
    k = k * (1.0 + (a - 1.0) * k_a)
    rh, kh, vh, ah = hs(r), hs(k), hs(v), hs(a)
    decay = jnp.exp(-jnp.exp(hs(w_log)))
    y = rwkv7_scan(rh, decay, kh, vh, -kk, kk * ah)
    mu = jnp.mean(y, axis=-1, keepdims=True)
    var = jnp.mean(jnp.square(y - mu), axis=-1, keepdims=True)
    y = ((y - mu) * lax.rsqrt(var + RW_GN_EPS)).reshape(B, T, D) * ln_w + ln_b
    bonus = (jnp.sum(rh * kh * r_k, axis=-1, keepdims=True) * vh).reshape(B, T, D)
    out = ((y + bonus).astype(x.dtype) * g) @ w_o
    return out, v_first


def gla_chunk(S, q, k, v, g):
    L = q.shape[2]
    b = jnp.cumsum(g, axis=2)
    b_last = b[:, :, -1:, :]
    q_t = q * jnp.exp(b)
    k_t = k * jnp.exp(-b)
    causal = jnp.tril(jnp.ones((L, L), dtype=bool))
    A = jnp.where(causal, jnp.einsum('bhid,bhjd->bhij', q_t, k_t), 0.0)
    o = jnp.einsum('bhid,bhdv->bhiv', q_t, S) + jnp.einsum('bhij,bhjv->bhiv', A, v)
    S_new = jnp.exp(b_last)[:, :, 0, :, None] * S + jnp.einsum('bhjd,bhjv->bhdv', k * jnp.exp(b_last - b), v)
    return S_new, o


def gla_time_mix(x, w_in, w_a2, b_a, gn_w, w_o):
    B, T, D = x.shape
    p = x @ w_in
    q = p[..., :GLA_DK]
    k = p[..., GLA_DK:2 * GLA_DK]
    v = p[..., 2 * GLA_DK:2 * GLA_DK + GLA_DV]
    gate = p[..., 2 * GLA_DK + GLA_DV:2 * GLA_DK + 2 * GLA_DV]
    za = p[..., 2 * GLA_DK + 2 * GLA_DV:]
    glog = jax.nn.log_sigmoid((za @ w_a2 + b_a).astype(jnp.float32)) / GLA_TAU
    heads = lambda t, d: t.reshape(B, T, GLA_HEADS, d).transpose(0, 2, 1, 3).astype(jnp.float32)
    qh = heads(q, GLA_HK) * (GLA_HK ** -0.5)
    kh, gh, vh = heads(k, GLA_HK), heads(glog, GLA_HK), heads(v, GLA_HV)
    S0 = jnp.zeros((B, GLA_HEADS, GLA_HK, GLA_HV), jnp.float32)
    S1, o_meta = gla_chunk(S0, qh[:, :, :N_META], kh[:, :, :N_META], vh[:, :, :N_META], gh[:, :, :N_META])
    n_c = (T - N_META) // GLA_CHUNK
    to_chunks = lambda t: jnp.moveaxis(t[:, :, N_META:].reshape(B, GLA_HEADS, n_c, GLA_CHUNK, t.shape[-1]), 2, 0)
    _, o_real = lax.scan(lambda S, c: gla_chunk(S, *c), S1,
                         (to_chunks(qh), to_chunks(kh), to_chunks(vh), to_chunks(gh)))
    o_real = jnp.moveaxis(o_real, 0, 2).reshape(B, GLA_HEADS, T - N_META, GLA_HV)
    o = jnp.concatenate([o_meta, o_real], axis=2)
    o = o * lax.rsqrt(jnp.mean(o * o, axis=-1, keepdims=True) + GLA_HEAD_EPS)
    o = o * gn_w.reshape(GLA_HEADS, 1, GLA_HV)
    o = o.transpose(0, 2, 1, 3).reshape(B, T, GLA_DV).astype(x.dtype)
    return (o * jax.nn.silu(gate)) @ w_o


def setup_inputs(seed: int = 0) -> dict:
    key = jax.random.key(seed)
    ks = iter(jax.random.split(key, 48))
    f32 = jnp.float32
    D = D_MODEL

    def nrm(shape, scale):
        return jax.random.normal(next(ks), shape, f32) * scale

    def unif(shape, lo, hi):
        return jax.random.uniform(next(ks), shape, f32, lo, hi)

    return {
        'x': nrm((BATCH, SEQ, D), 1.0),
        'meta': nrm((N_META, D), 1.0),
        'norm_mix': 1.0 + nrm((DEPTH, D), 0.05),
        'norm_mlp': 1.0 + nrm((DEPTH, D), 0.05),
        'norm_f': 1.0 + nrm((D,), 0.05),
        'mlp_w1': nrm((DEPTH, D, D_FF), D ** -0.5),
        'mlp_w2': nrm((DEPTH, D_FF, D), D_FF ** -0.5),
        'rw_mix': unif((N_RWKV, 6, D), 0.0, 1.0),
        'rw_w_rkv': nrm((N_RWKV, 3, D, D), D ** -0.5),
        'rw_w0': unif((N_RWKV, D), -6.0, -0.5),
        'rw_w1': nrm((N_RWKV, D, RW_DECAY_LORA), D ** -0.5),
        'rw_w2': nrm((N_RWKV, RW_DECAY_LORA, D), 0.5 * RW_DECAY_LORA ** -0.5),
        'rw_a0': nrm((N_RWKV, D), 0.1),
        'rw_a1': nrm((N_RWKV, D, RW_AAA_LORA), D ** -0.5),
        'rw_a2': nrm((N_RWKV, RW_AAA_LORA, D), RW_AAA_LORA ** -0.5),
        'rw_v0': nrm((N_RWKV - 1, D), 0.1),
        'rw_v1': nrm((N_RWKV - 1, D, RW_MV_LORA), D ** -0.5),
        'rw_v2': nrm((N_RWKV - 1, RW_MV_LORA, D), RW_MV_LORA ** -0.5),
        'rw_g1': nrm((N_RWKV, D, RW_GATE_LORA), D ** -0.5),
        'rw_g2': nrm((N_RWKV, RW_GATE_LORA, D), RW_GATE_LORA ** -0.5),
        'rw_k_k': 0.85 + nrm((N_RWKV, D), 0.05),
        'rw_k_a': 1.0 + nrm((N_RWKV, D), 0.05),
        'rw_r_k': nrm((N_RWKV, RW_HEADS, RW_HEAD), 0.1),
        'rw_ln_w': 1.0 + nrm((N_RWKV, D), 0.05),
        'rw_ln_b': nrm((N_RWKV, D), 0.01),
        'rw_w_o': nrm((N_RWKV, D, D), D ** -0.5),
        'gla_w_in': nrm((N_GLA, D, GLA_IN), D ** -0.5),
        'gla_w_a2': nrm((N_GLA, GLA_GATE_LORA, GLA_DK), GLA_GATE_LORA ** -0.5),
        'gla_b_a': nrm((N_GLA, GLA_DK), 0.1),
        'gla_gn_w': 1.0 + nrm((N_GLA, GLA_DV), 0.05),
        'gla_w_o': nrm((N_GLA, GLA_DV, D), GLA_DV ** -0.5),
    }


def reference(x, meta, norm_mix, norm_mlp, norm_f, mlp_w1, mlp_w2, rw_mix, rw_w_rkv, rw_w0, rw_w1,
              rw_w2, rw_a0, rw_a1, rw_a2, rw_v0, rw_v1, rw_v2, rw_g1, rw_g2, rw_k_k, rw_k_a, rw_r_k,
              rw_ln_w, rw_ln_b, rw_w_o, gla_w_in, gla_w_a2, gla_b_a, gla_gn_w, gla_w_o):
    B = x.shape[0]
    h = jnp.concatenate([jnp.broadcast_to(meta.astype(x.dtype)[None], (B, N_META, D_MODEL)), x], axis=1)
    v_first = None
    for i in range(DEPTH):
        j = i // N_MIXERS
        hn = rms_norm(h, norm_mix[i])
        if i % N_MIXERS == 0:
            vres = None if j == 0 else (rw_v0[j - 1], rw_v1[j - 1], rw_v2[j - 1])
            mix_out, vf = rwkv7_time_mix(hn, v_first, rw_mix[j], rw_w_rkv[j], rw_w0[j], rw_w1[j], rw_w2[j],
                                         rw_a0[j], rw_a1[j], rw_a2[j], rw_g1[j], rw_g2[j], rw_k_k[j],
                                         rw_k_a[j], rw_r_k[j], rw_ln_w[j], rw_ln_b[j], rw_w_o[j], vres)
            if j == 0:
                v_first = vf
        else:
            mix_out = gla_time_mix(hn, gla_w_in[j], gla_w_a2[j], gla_b_a[j], gla_gn_w[j], gla_w_o[j])
        h = h + mix_out
        h = h + sq_relu_mlp(rms_norm(h, norm_mlp[i]), mlp_w1[i], mlp_w2[i])
    return rms_norm(h, norm_f)[:, N_META:]
```

```python
import os, math
from concourse.bass_utils import run_bass_kernel_spmd
import numpy as np
import concourse.bass as bass
import concourse.mybir as mybir
from contextlib import ExitStack

F32 = mybir.dt.float32
BF16 = mybir.dt.bfloat16
AF = mybir.ActivationFunctionType
ALU = mybir.AluOpType
AX = mybir.AxisListType


class Buf:
    __slots__ = ("w", "r", "name")

    def __init__(self, name=""):
        self.w = None
        self.r = []
        self.name = name


class T:
    def __init__(self, h, name):
        self.h = h
        self.name = name
        self.buf = Buf(name)
        self.subs = {}

    def __getitem__(self, idx):
        return self.h[idx]

    def sub(self, key):
        b = self.subs.get(key)
        if b is None:
            b = Buf(f"{self.name}.{key}")
            self.subs[key] = b
        return b


def _bufs(xs):
    out = []
    for x in xs:
        if x is None:
            continue
        if isinstance(x, T):
            out.append(x.buf)
        elif isinstance(x, Buf):
            out.append(x)
        else:
            raise TypeError(x)
    return out


class Prog:
    DMA_ENGS = ("sp", "pool", "act_q")
    NS = 12

    def __init__(self, nc, es: ExitStack):
        self.nc = nc
        self.es = es
        self.eng = {"pe": nc.tensor, "act": nc.scalar, "dve": nc.vector,
                    "pool": nc.gpsimd, "sp": nc.sync}
        self.sems = {}
        self.cnt = {}
        for e in ("pe", "act", "dve", "pool"):
            self.sems[e] = es.enter_context(nc.semaphore("s_" + e))
            self.cnt[e] = 0
        self.dq = {}
        for q in ("sp", "pool", "act"):
            sl = []
            for i in range(self.NS):
                key = f"d_{q}{i}"
                self.sems[key] = es.enter_context(nc.semaphore(key))
                sl.append(key)
            self.dq[q] = {"sems": sl, "n": 0}
        self.seen = {e: {} for e in self.eng}
        self.n_inst = 0
        self.n_wait = 0
        self.uid = 0

    def sb(self, name, shape, dt):
        self.uid += 1
        h = self.es.enter_context(self.nc.sbuf_tensor(f"{name}_{self.uid}", list(shape), dt))
        return T(h, name)

    def ps(self, name, shape=(128, 512), dt=F32):
        self.uid += 1
        h = self.es.enter_context(self.nc.psum_tensor(f"{name}_{self.uid}", list(shape), dt))
        return T(h, name)

    def dram(self, name, shape, dt, kind="Internal"):
        h = self.nc.dram_tensor(name, list(shape), dt, kind=kind)
        t = T(h.ap(), name)
        return t

    def _wait(self, e, ev):
        key, val, src = ev
        s = self.seen[e]
        if s.get(key, 0) >= val:
            return
        s[key] = val
        self.eng[e].wait_ge(self.sems[key], val)
        self.n_wait += 1

    def _deps(self, e, reads, writes, is_dma):
        deps = []
        for b in reads:
            if b.w is not None:
                deps.append(b.w)
        for b in writes:
            if b.w is not None and (e != "pe" or b.w[2] != "pe"):
                deps.append(b.w)
            for r in b.r:
                deps.append(r)
        for d in deps:
            self._wait(e, d)

    def _commit(self, ev, reads, writes):
        for b in reads:
            b.r.append(ev)
            if len(b.r) > 24:
                m = {}
                for k, v, s in b.r:
                    if m.get(k, (0, None))[0] < v:
                        m[k] = (v, s)
                b.r = [(k, v, s) for k, (v, s) in m.items()]
        for b in writes:
            b.w = ev
            b.r = []

    def op(self, e, fn, reads=(), writes=()):
        reads = _bufs(reads)
        writes = _bufs(writes)
        self._deps(e, reads, writes, False)
        ins = fn()
        self.cnt[e] += 1
        ins.then_inc(self.sems[e], 1)
        ev = (e, self.cnt[e], e)
        self._commit(ev, reads, writes)
        self.n_inst += 1
        return ev

    def mm(self, out_ap, pairs, reads, writes, start=True, stop=True, after=None):
        reads = _bufs(reads)
        writes = _bufs(writes)
        self._deps("pe", reads, writes, False)
        if after is not None:
            self._wait("pe", after)
        n = len(pairs)
        ins = None
        for i, (l, r) in enumerate(pairs):
            ins = self.nc.tensor.matmul(out_ap, lhsT=l, rhs=r,
                                        start=(start and i == 0), stop=(stop and i == n - 1))
            self.n_inst += 1
        self.cnt["pe"] += 1
        ins.then_inc(self.sems["pe"], 1)
        ev = ("pe", self.cnt["pe"], "pe")
        self._commit(ev, reads, writes)
        return ev

    def pe_multi(self, fn, reads, writes):
        reads = _bufs(reads)
        writes = _bufs(writes)
        self._deps("pe", reads, writes, False)
        ins = fn()
        self.cnt["pe"] += 1
        ins.then_inc(self.sems["pe"], 1)
        ev = ("pe", self.cnt["pe"], "pe")
        self._commit(ev, reads, writes)
        return ev

    def dma(self, q, out_ap, in_ap, reads=(), writes=(), **kw):
        reads = _bufs(reads)
        writes = _bufs(writes)
        self._deps(q, reads, writes, True)
        d = self.dq[q]
        j = d["n"]
        d["n"] += 1
        key = d["sems"][j % self.NS]
        k = j // self.NS + 1
        if k > 1:
            self._wait(q, (key, 16 * (k - 1), q))
        ins = self.eng[q].dma_start(out=out_ap, in_=in_ap, **kw)
        ins.then_inc(self.sems[key], 16)
        ev = (key, 16 * k, q)
        self._commit(ev, reads, writes)
        self.n_inst += 1
        return ev

    def finish(self, out_tensors, q="sp"):
        for t in out_tensors:
            bs = [t.buf] + list(t.subs.values())
            for b in bs:
                if b.w is not None:
                    self._wait(q, b.w)

HT = 257
TT = 2 * HT


class Ctx:
    def __init__(self, p, nc, nbanks=8):
        self.p = p
        self.nc = nc
        self.banks = [p.ps(f"bank{i}") for i in range(nbanks)]
        self.nb = nbanks
        self.bi = 0
        self.ones_bf = p.sb("ones_bf", [128, 128], BF16)
        p.op("pool", lambda: nc.gpsimd.memset(self.ones_bf[:], 1.0), writes=[self.ones_bf])

    def bank(self):
        b = self.banks[self.bi % self.nb]
        self.bi += 1
        return b


def emit_rstd(cx, H, SQ, RS, D, eps, hsubs, sqsubs):
    p, nc = cx.p, cx.nc
    KC = D // 128
    for k in range(KC):
        p.op("act", lambda: nc.scalar.activation(out=SQ[:, k, :], in_=H[:, k, :], func=AF.Square),
             reads=[hsubs(k)], writes=[sqsubs(k)])
    for half in range(2):
        sl = slice(half * HT, (half + 1) * HT)
        bk = cx.bank()
        p.mm(bk[:, 0:HT], [(cx.ones_bf[:], SQ[:, k, sl]) for k in range(KC)],
             reads=[cx.ones_bf] + [sqsubs(k) for k in range(KC)], writes=[bk])
        p.op("act", lambda: nc.scalar.activation(out=RS[:, sl], in_=bk[:, 0:HT], func=AF.Sqrt,
                                                 scale=1.0 / D, bias=eps),
             reads=[bk], writes=[RS])
    p.op("dve", lambda: nc.vector.reciprocal(out=RS[:], in_=RS[:]), reads=[RS], writes=[RS])


def stage3(cx, io, NT, D, F, last, do_wo=True):
    p, nc = cx.p, cx.nc
    KC = D // 128
    FC = F // 128
    FG = 4
    ntile = NT // TT
    A = p.sb("A", [128, KC, TT], BF16)
    H = p.sb("H", [128, KC, TT], F32)
    HID = p.sb("HID", [128, FC, TT], BF16)
    RS = p.sb("RS", [128, TT], F32)
    gm = p.sb("gm", [128, KC], F32)
    gn = p.sb("gn", [128, KC], F32)
    tmpr = [p.sb(f"tmpr{i}", [128, HT], F32) for i in range(3)]
    wob = [p.sb(f"wob{i}", [128, KC, 128], BF16) for i in range(2)]
    w1b = [p.sb(f"w1b{i}", [128, FG, KC, 128], BF16) for i in range(2)]
    w2b = [p.sb(f"w2b{i}", [128, FC, 128], BF16) for i in range(2)]
    p.dma("pool", gm[:], io["g_mlp"][:], writes=[gm])
    p.dma("pool", gn[:], io["g_next"][:], writes=[gn])
    Asub = lambda h: A.sub(h)
    Hsub = lambda k: H.sub(k)
    HIDsub = lambda f: HID.sub(f)
    allA = [Asub(0), Asub(1)]
    allH = [Hsub(k) for k in range(KC)]
    wi = [0, 0, 0, 0]
    for t in range(ntile):
        ts_ = slice(t * TT, (t + 1) * TT)
        p.dma("pool", H[:], io["h_in"][:, :, ts_], writes=allH)
        if do_wo:
            p.dma("pool", A[:], io["yg"][:, :, ts_], writes=allA)
            for dc in range(KC):
                wb = wob[wi[0] % 2]; wi[0] += 1
                p.dma("sp", wb[:], io["wo"][dc], writes=[wb])
                for half in range(2):
                    sl = slice(half * HT, (half + 1) * HT)
                    bk = cx.bank()
                    p.mm(bk[:, 0:HT], [(wb[:, k, :], A[:, k, sl]) for k in range(KC)],
                         reads=[wb, Asub(half)], writes=[bk])
                    p.op("dve", lambda: nc.vector.tensor_tensor(out=H[:, dc, sl], in0=H[:, dc, sl],
                                                                in1=bk[:, 0:HT], op=ALU.add),
                         reads=[bk, Hsub(dc)], writes=[Hsub(dc)])
        emit_rstd(cx, H, HID, RS, D, 1e-6, Hsub, HIDsub)
        for k in range(KC):
            p.op("dve", lambda: nc.vector.scalar_tensor_tensor(out=A[:, k, :], in0=H[:, k, :],
                                                               scalar=gm[:, k:k + 1], in1=RS[:],
                                                               op0=ALU.mult, op1=ALU.mult),
                 reads=[Hsub(k), gm, RS], writes=allA)
        for fg in range(FC // FG):
            wb = w1b[wi[1] % 2]; wi[1] += 1
            p.dma("sp", wb[:], io["w1"][fg * FG:(fg + 1) * FG].rearrange("g p k f -> p g k f"), writes=[wb])
            for j in range(FG):
                fc = fg * FG + j
                for half in range(2):
                    sl = slice(half * HT, (half + 1) * HT)
                    bk = cx.bank()
                    p.mm(bk[:, 0:HT], [(wb[:, j, k, :], A[:, k, sl]) for k in range(KC)],
                         reads=[wb, Asub(half)], writes=[bk])
                    tr = tmpr[wi[3] % 3]; wi[3] += 1
                    p.op("act", lambda: nc.scalar.activation(out=tr[:], in_=bk[:, 0:HT], func=AF.Relu),
                         reads=[bk], writes=[tr])
                    p.op("dve", lambda: nc.vector.tensor_tensor(out=HID[:, fc, sl], in0=tr[:], in1=tr[:],
                                                                op=ALU.mult),
                         reads=[tr], writes=[HIDsub(fc)])
        for dc in range(KC):
            wb = w2b[wi[2] % 2]; wi[2] += 1
            p.dma("sp", wb[:], io["w2"][dc], writes=[wb])
            for half in range(2):
                sl = slice(half * HT, (half + 1) * HT)
                bk = cx.bank()
                p.mm(bk[:, 0:HT], [(wb[:, f, :], HID[:, f, sl]) for f in range(FC)],
                     reads=[wb] + [HIDsub(f) for f in range(FC)], writes=[bk])
                p.op("dve", lambda: nc.vector.tensor_tensor(out=H[:, dc, sl], in0=H[:, dc, sl],
                                                            in1=bk[:, 0:HT], op=ALU.add),
                     reads=[bk, Hsub(dc)], writes=[Hsub(dc)])
        if not last:
            p.dma("pool", io["h_out"][:, :, ts_], H[:], reads=allH, writes=[io["h_out"].sub(t)])
        emit_rstd(cx, H, HID, RS, D, 1e-6, Hsub, HIDsub)
        if not last:
            for k in range(KC):
                p.op("dve", lambda: nc.vector.scalar_tensor_tensor(out=A[:, k, :], in0=H[:, k, :],
                                                                   scalar=gn[:, k:k + 1], in1=RS[:],
                                                                   op0=ALU.mult, op1=ALU.mult),
                     reads=[Hsub(k), gn, RS], writes=allA)
            p.dma("pool", io["hn_out"][:, :, ts_], A[:], reads=allA, writes=[io["hn_out"].sub(t)])
        else:
            for k in range(KC):
                p.op("dve", lambda: nc.vector.scalar_tensor_tensor(out=H[:, k, :], in0=H[:, k, :],
                                                                   scalar=gn[:, k:k + 1], in1=RS[:],
                                                                   op0=ALU.mult, op1=ALU.mult),
                     reads=[Hsub(k), gn, RS], writes=[Hsub(k)])
            p.dma("pool", io["out"][:, :, ts_], H[:], reads=allH, writes=[io["out"].sub(t)])


def stage_norm(cx, io, NT, D):
    p, nc = cx.p, cx.nc
    KC = D // 128
    H = p.sb("H", [128, KC, TT], F32)
    A = p.sb("A", [128, KC, TT], BF16)
    SQ = p.sb("SQ", [128, KC, TT], BF16)
    RS = p.sb("RS", [128, TT], F32)
    gn = p.sb("gn", [128, KC], F32)
    p.dma("pool", gn[:], io["g_next"][:], writes=[gn])
    for t in range(NT // TT):
        ts_ = slice(t * TT, (t + 1) * TT)
        p.dma("sp", H[:], io["h_in"][:, :, ts_], writes=[H])
        emit_rstd(cx, H, SQ, RS, D, 1e-6, lambda k: H.buf, lambda k: SQ.buf)
        for k in range(KC):
            p.op("dve", lambda: nc.vector.scalar_tensor_tensor(out=A[:, k, :], in0=H[:, k, :],
                                                               scalar=gn[:, k:k + 1], in1=RS[:],
                                                               op0=ALU.mult, op1=ALU.mult),
                 reads=[H, gn, RS], writes=[A])
        p.dma("pool", io["hn_out"][:, :, ts_], A[:], reads=[A], writes=[io["hn_out"].sub(t)])

SEQ_T = 4112


def seq_tiles(T=SEQ_T):
    tiles = []
    assert (T - 16) % 64 == 0
    nreg = (T - 16) // 64
    first = min(7, nreg)
    tiles.append((0, [16] + [64] * first))
    done = first
    t0 = 16 + 64 * first
    while done < nreg:
        n = min(8, nreg - done)
        tiles.append((t0, [64] * n))
        t0 += 64 * n
        done += n
    return tiles


def make_masks(cx, T):
    p, nc = cx.p, cx.nc
    rm = p.sb("rmask", [128, T], BF16)
    p.op("pool", lambda: nc.gpsimd.memset(rm[:], 1.0), writes=[rm])
    p.op("pool", lambda: nc.gpsimd.memset(rm[:, 0:1], 0.0), writes=[rm])
    p.op("pool", lambda: nc.gpsimd.memset(rm[:, 16:T:64], 0.0), writes=[rm])
    cx.rmask = rm


def stage_gla(cx, io, T):
    p, nc = cx.p, cx.nc
    KC = 16
    HK = 256
    WQ = p.sb("WQ", [128, KC, 512], BF16)
    WK = p.sb("WK", [128, KC, 512], BF16)
    WV = p.sb("WV", [128, KC, 1024], BF16)
    WG = p.sb("WG", [128, KC, 1024], BF16)
    WZA = p.sb("WZA", [128, KC, 16], BF16)
    WA2 = p.sb("WA2", [16, 512], BF16)
    NBA = p.sb("NBA", [128, 4], F32)
    GNW = p.sb("GNW", [64, 1024], F32)
    MI = p.sb("MI", [64, 64], F32)
    ID = p.sb("ID", [128, 128], BF16)
    for t_, k_ in ((WQ, "wq"), (WK, "wk"), (WV, "wv"), (WG, "wg"), (WZA, "wza"), (WA2, "wa2"),
                   (NBA, "ba"), (MI, "mask_incl"), (ID, "ident")):
        p.dma("sp", t_[:], io[k_][:], writes=[t_])
    p.op("dve", lambda: nc.vector.tensor_scalar(out=NBA[:], in0=NBA[:], scalar1=-1.0, scalar2=None, op0=ALU.mult),
         reads=[NBA], writes=[NBA])
    p.dma("sp", GNW[:], io["gnw"][0:1, :].partition_broadcast(64), writes=[GNW])
    make_masks(cx, T)
    rm = cx.rmask
    TM = 512
    HN = [p.sb(f"HN{i}", [128, KC, TM], BF16) for i in range(1)]
    ZA = p.sb("ZA", [16, TM], BF16)
    E0 = p.sb("E0", [128, 4, TM], F32)
    L = p.sb("L", [128, 4, TM], F32)
    BC = p.sb("BC", [128, 4, TM], F32)
    QE = p.sb("QE", [128, 4, TM], BF16)
    KE = p.sb("KE", [128, 4, TM], BF16)
    KB = p.sb("KB", [128, 4, TM], BF16)
    DC = p.sb("DC", [128, 4, 8], F32)
    OT = [p.sb(f"OT{i}", [128, 8, TM], BF16) for i in range(1)]
    S = [p.sb(f"S{h}", [128, 2, 512], F32) for h in range(2)]
    Sb = [p.sb(f"Sb{h}", [128, 2, 512], BF16) for h in range(2)]
    for h in range(2):
        p.op("pool", lambda: nc.gpsimd.memset(S[h][:], 0.0), writes=[S[h]])
        p.op("pool", lambda: nc.gpsimd.memset(Sb[h][:], 0.0), writes=[Sb[h]])
    AT = [p.sb(f"AT{i}", [64, 64], BF16) for i in range(2)]
    Vb = [p.sb(f"Vb{i}", [64, 512], BF16) for i in range(2)]
    EG = [p.sb(f"EG{i}", [64, 512], F32) for i in range(2)]
    SG = [p.sb(f"SG{i}", [64, 512], F32) for i in range(2)]
    ON = [p.sb(f"ON{i}", [64, 512], F32) for i in range(2)]
    OSQ = p.sb("OSQ", [64, 512], F32)
    OG = [p.sb(f"OG{i}", [64, 512], BF16) for i in range(2)]
    KBt = [p.sb(f"KBt{i}", [64, 256], BF16) for i in range(2)]
    ST = [p.sb(f"ST{i}", [64, 4], F32) for i in range(2)]
    trb = [p.ps(f"trb{i}", [128, 1024], BF16) for i in range(2)]
    banks = [p.ps(f"gb{i}") for i in range(6)]
    bi = [0]

    def bank():
        b = banks[bi[0] % 6]
        bi[0] += 1
        return b
    ci = 0
    tiles = seq_tiles(T)
    for ti, (t0, chunks) in enumerate(tiles):
        Tt = sum(chunks)
        hn = HN[0]
        ot = OT[0]
        p.dma("pool", hn[:, :, 0:Tt], io["hn"][:, :, t0:t0 + Tt], writes=[hn])
        bk = bank()
        p.mm(bk[0:16, 0:Tt], [(WZA[:, k, :], hn[:, k, 0:Tt]) for k in range(KC)], reads=[WZA, hn], writes=[bk])
        p.op("act", lambda: nc.scalar.copy(out=ZA[:, 0:Tt], in_=bk[0:16, 0:Tt]), reads=[bk], writes=[ZA])
        for cc in range(4):
            bk = bank()
            p.mm(bk[:, 0:Tt], [(WA2[:, cc * 128:(cc + 1) * 128], ZA[:, 0:Tt])], reads=[WA2, ZA], writes=[bk])
            p.op("act", lambda: nc.scalar.activation(out=E0[:, cc, 0:Tt], in_=bk[:, 0:Tt], func=AF.Exp,
                                                     scale=-1.0, bias=NBA[:, cc:cc + 1]),
                 reads=[bk, NBA], writes=[E0])
        p.op("act", lambda: nc.scalar.activation(out=L[:, :, 0:Tt], in_=E0[:, :, 0:Tt], func=AF.Ln, bias=1.0),
             reads=[E0], writes=[L])
        for cc in range(4):
            p.op("dve", lambda: nc.vector.tensor_tensor_scan(out=BC[:, cc, 0:Tt], data0=rm[:, t0:t0 + Tt],
                                                             data1=L[:, cc, 0:Tt], initial=0.0,
                                                             op0=ALU.mult, op1=ALU.add),
                 reads=[rm, L], writes=[BC])
        off = 0
        for j, C in enumerate(chunks):
            p.op("act", lambda: nc.scalar.activation(out=DC[:, :, j:j + 1], in_=BC[:, :, off + C - 1:off + C],
                                                     func=AF.Exp, scale=-1.0 / 16),
                 reads=[BC], writes=[DC])
            p.op("dve", lambda: nc.vector.tensor_tensor(
                out=E0[:, :, off:off + C], in0=BC[:, :, off:off + C],
                in1=BC[:, :, off + C - 1:off + C].to_broadcast([128, 4, C]), op=ALU.subtract),
                reads=[BC], writes=[E0])
            off += C
        p.op("act", lambda: nc.scalar.activation(out=E0[:, :, 0:Tt], in_=E0[:, :, 0:Tt], func=AF.Exp, scale=1.0 / 16),
             reads=[E0], writes=[E0])
        p.op("act", lambda: nc.scalar.activation(out=L[:, :, 0:Tt], in_=BC[:, :, 0:Tt], func=AF.Exp, scale=-1.0 / 16),
             reads=[BC], writes=[L])
        for cc in range(4):
            bk = bank()
            p.mm(bk[:, 0:Tt], [(WQ[:, k, cc * 128:(cc + 1) * 128], hn[:, k, 0:Tt]) for k in range(KC)],
                 reads=[WQ, hn], writes=[bk])
            p.op("dve", lambda: nc.vector.scalar_tensor_tensor(out=QE[:, cc, 0:Tt], in0=bk[:, 0:Tt],
                                                               scalar=float(HK) ** -0.5, in1=L[:, cc, 0:Tt],
                                                               op0=ALU.mult, op1=ALU.mult),
                 reads=[bk, L], writes=[QE])
        p.op("act", lambda: nc.scalar.activation(out=L[:, :, 0:Tt], in_=BC[:, :, 0:Tt], func=AF.Exp, scale=1.0 / 16),
             reads=[BC], writes=[L])
        for cc in range(4):
            bk = bank()
            p.mm(bk[:, 0:Tt], [(WK[:, k, cc * 128:(cc + 1) * 128], hn[:, k, 0:Tt]) for k in range(KC)],
                 reads=[WK, hn], writes=[bk])
            p.op("dve", lambda: nc.vector.tensor_tensor(out=KE[:, cc, 0:Tt], in0=bk[:, 0:Tt], in1=L[:, cc, 0:Tt],
                                                        op=ALU.mult), reads=[bk, L], writes=[KE])
            p.op("dve", lambda: nc.vector.tensor_tensor(out=KB[:, cc, 0:Tt], in0=bk[:, 0:Tt], in1=E0[:, cc, 0:Tt],
                                                        op=ALU.mult), reads=[bk, E0], writes=[KB])
        off = 0
        for j, C in enumerate(chunks):
            cs = slice(off, off + C)
            for hd in range(2):
                u = ci % 2
                ci += 1
                at, vb, eg, sg, on, og, kbt, st = AT[u], Vb[u], EG[u], SG[u], ON[u], OG[u], KBt[u], ST[u]
                bk = bank()
                p.mm(bk[0:C, 0:C], [(KE[:, 2 * hd + kc, cs], QE[:, 2 * hd + kc, cs]) for kc in range(2)],
                     reads=[KE, QE], writes=[bk])
                p.op("dve", lambda: nc.vector.tensor_tensor(out=at[0:C, 0:C], in0=bk[0:C, 0:C], in1=MI[0:C, 0:C],
                                                            op=ALU.mult), reads=[bk, MI], writes=[at])
                bk = bank()
                p.mm(bk[0:C, :], [(hn[:, k, cs], WV[:, k, hd * 512:(hd + 1) * 512]) for k in range(KC)],
                     reads=[hn, WV], writes=[bk])
                p.op("act", lambda: nc.scalar.copy(out=vb[0:C, :], in_=bk[0:C, :]), reads=[bk], writes=[vb])
                bkg = bank()
                p.mm(bkg[0:C, :], [(hn[:, k, cs], WG[:, k, hd * 512:(hd + 1) * 512]) for k in range(KC)],
                     reads=[hn, WG], writes=[bkg])
                p.op("act", lambda: nc.scalar.activation(out=eg[0:C, :], in_=bkg[0:C, :], func=AF.Exp, scale=-1.0),
                     reads=[bkg], writes=[eg])
                p.op("dve", lambda: nc.vector.tensor_scalar(out=eg[0:C, :], in0=eg[0:C, :], scalar1=1.0, scalar2=None,
                                                            op0=ALU.add), reads=[eg], writes=[eg])
                p.op("dve", lambda: nc.vector.reciprocal(out=eg[0:C, :], in_=eg[0:C, :]), reads=[eg], writes=[eg])
                p.op("dve", lambda: nc.vector.tensor_tensor(out=sg[0:C, :], in0=bkg[0:C, :], in1=eg[0:C, :],
                                                            op=ALU.mult), reads=[bkg, eg], writes=[sg])
                bko = bank()
                p.mm(bko[0:C, :], [(QE[:, 2 * hd + kc, cs], Sb[hd][:, kc, :]) for kc in range(2)]
                     + [(at[0:C, 0:C], vb[0:C, :])], reads=[QE, Sb[hd], at, vb], writes=[bko])
                p.op("act", lambda: nc.scalar.activation(out=OSQ[0:C, :], in_=bko[0:C, :], func=AF.Square,
                                                         accum_out=st[0:C, 0:1]), reads=[bko], writes=[OSQ, st])
                p.op("act", lambda: nc.scalar.activation(out=st[0:C, 1:2], in_=st[0:C, 0:1], func=AF.Ln,
                                                         scale=1.0 / 512, bias=1e-5), reads=[st], writes=[st])
                p.op("act", lambda: nc.scalar.activation(out=st[0:C, 2:3], in_=st[0:C, 1:2], func=AF.Exp, scale=-0.5),
                     reads=[st], writes=[st])
                p.op("dve", lambda: nc.vector.scalar_tensor_tensor(out=on[0:C, :], in0=bko[0:C, :], scalar=st[0:C, 2:3],
                                                                   in1=GNW[0:C, hd * 512:(hd + 1) * 512],
                                                                   op0=ALU.mult, op1=ALU.mult),
                     reads=[bko, st, GNW], writes=[on])
                p.op("dve", lambda: nc.vector.tensor_tensor(out=og[0:C, :], in0=on[0:C, :], in1=sg[0:C, :], op=ALU.mult),
                     reads=[on, sg], writes=[og])
                tb = trb[u]

                def tr4():
                    ins = None
                    for vc in range(4):
                        ins = nc.tensor.transpose(tb[:, vc * 64:vc * 64 + C], og[0:C, vc * 128:(vc + 1) * 128],
                                                  ID[0:C, 0:C])
                    return ins
                p.pe_multi(tr4, reads=[og, ID], writes=[tb])
                p.op("act", lambda: nc.scalar.copy(
                    out=ot[:, hd * 4:(hd + 1) * 4, cs],
                    in_=tb[:, 0:256].rearrange("p (v t) -> p v t", t=64)[:, :, 0:C]), reads=[tb], writes=[ot])

                def tr2():
                    ins = None
                    for kc in range(2):
                        ins = nc.tensor.transpose(tb[0:C, 512 + kc * 128:512 + (kc + 1) * 128],
                                                  KB[:, 2 * hd + kc, cs], ID[:, :])
                    return ins
                p.pe_multi(tr2, reads=[KB, ID], writes=[tb])
                p.op("act", lambda: nc.scalar.copy(out=kbt[0:C, :], in_=tb[0:C, 512:768]), reads=[tb], writes=[kbt])
                for kc in range(2):
                    bks = bank()
                    p.mm(bks[:, :], [(kbt[0:C, kc * 128:(kc + 1) * 128], vb[0:C, :])], reads=[kbt, vb], writes=[bks])
                    p.op("dve", lambda: nc.vector.scalar_tensor_tensor(
                        out=S[hd][:, kc, :], in0=S[hd][:, kc, :], scalar=DC[:, 2 * hd + kc, j:j + 1], in1=bks[:, :],
                        op0=ALU.mult, op1=ALU.add), reads=[S[hd], DC, bks], writes=[S[hd]])
                    p.op("act", lambda: nc.scalar.copy(out=Sb[hd][:, kc, :], in_=S[hd][:, kc, :]),
                         reads=[S[hd]], writes=[Sb[hd]])
            off += C
        p.dma("pool", io["og"][:, :, t0:t0 + Tt], ot[:, :, 0:Tt], reads=[ot], writes=[io["og"].sub(ti)])

CW = math.exp(-0.5)
PADL = 48


def rw_tiles(T, first=3, per=4):
    nreg = (T - 16) // 64
    tiles = [(0, [16] + [64] * min(first, nreg))]
    done = min(first, nreg)
    t0 = 16 + 64 * done
    while done < nreg:
        n = min(per, nreg - done)
        tiles.append((t0, [64] * n))
        t0 += 64 * n
        done += n
    return tiles


def barrier(p):
    evs = []
    for e in ("pe", "act", "dve", "pool"):
        if p.cnt[e] > 0:
            evs.append((e, p.cnt[e], e))
    for q, d in p.dq.items():
        for i, key in enumerate(d["sems"]):
            n = d["n"]
            k = (n - 1 - i) // p.NS + 1 if n > i else 0
            if k > 0:
                evs.append((key, 16 * k, q))
    for e in p.eng:
        for ev in evs:
            p._wait(e, ev)


def stage_rw(cx, io, T, layer2):
    p, nc = cx.p, cx.nc
    KC = 16
    TP = PADL + T
    NCH = TP // 64
    es_outer = p.es
    banks = cx.banks
    bi = [0]

    def bank():
        b = banks[bi[0] % len(banks)]
        bi[0] += 1
        return b

    with ExitStack() as esA:
        p.es = esA
        TM = 256
        W1 = p.sb("W1", [128, KC, 96], BF16)
        A1 = p.sb("A1", [128, KC, 96], BF16)
        G1 = p.sb("G1", [128, KC, 256], BF16)
        W2 = p.sb("W2", [96, 1024], BF16)
        A2 = p.sb("A2", [96, 1024], BF16)
        G2 = p.sb("G2", [128, 2, 1024], BF16)
        MIX = p.sb("MIX", [128, 6, KC], F32)
        PRM = p.sb("PRM", [128, 8, 8], F32)
        NPR = p.sb("NPR", [128, 3, 8], F32)
        BLK = p.sb("BLK", [128, 128], BF16)
        ld = [(W1, "w1"), (A1, "a1"), (G1, "g1"), (W2, "w2"), (A2, "a2"), (G2, "g2"), (MIX, "mix"), (PRM, "prm"),
              (BLK, "blk")]
        if layer2:
            V1 = p.sb("V1", [128, KC, 64], BF16)
            V2 = p.sb("V2", [64, 1024], BF16)
            ld += [(V1, "v1"), (V2, "v2")]
        for t_, k_ in ld:
            p.dma("sp", t_[:], io[k_][:], writes=[t_])
        p.op("dve", lambda: nc.vector.tensor_scalar(out=NPR[:], in0=PRM[:, 0:3, :], scalar1=-1.0, scalar2=None,
                                                    op0=ALU.mult), reads=[PRM], writes=[NPR])
        rm = p.sb("rmask", [128, T], BF16)
        p.op("pool", lambda: nc.gpsimd.memset(rm[:], 1.0), writes=[rm])
        p.op("pool", lambda: nc.gpsimd.memset(rm[:, 0:1], 0.0), writes=[rm])
        p.op("pool", lambda: nc.gpsimd.memset(rm[:, 16:T:64], 0.0), writes=[rm])
        ZT = p.sb("ZT", [128, 2, PADL], BF16)
        p.op("pool", lambda: nc.gpsimd.memset(ZT[:], 0.0), writes=[ZT])
        for cc in range(8):
            for nm in ("ARs", "BKs", "BKBs"):
                p.dma("pool", io[nm][cc, :, :, 0:PADL], ZT[:], reads=[ZT], writes=[io[nm].sub(("z", cc))])
            p.dma("pool", io["VTs"][cc, :, 64:64 + PADL], ZT[:, 0, :], reads=[ZT], writes=[io["VTs"].sub(("z", cc))])
        XT = p.sb("XT", [128, KC, TM + 1], BF16)
        DX = p.sb("DX", [128, KC, TM], BF16)
        XM = [p.sb(f"XM{i}", [128, KC, TM], BF16) for i in range(2)]
        WS = [p.sb(f"WS{i}", [128, KC, 128], BF16) for i in range(3)]
        Fz = [p.sb(f"F{i}", [128, 8, TM], F32) for i in range(8)]
        SQb = p.sb("SQb", [128, 8, TM], BF16)
        LH = p.sb("LH", [128, TM], F32)
        LHb = p.sb("LHb", [128, 2, TM], BF16)
        ARt = p.sb("ARt", [128, 8, 2, TM], BF16)
        BKt = p.sb("BKt", [128, 8, 2, TM], BF16)
        BKBt = p.sb("BKBt", [128, 8, 2, TM], BF16)
        BGt = p.sb("BGt", [128, 8, 2, TM], BF16)
        VTt = p.sb("VTt", [128, 8, TM], BF16)
        PCt = p.sb("PCt", [128, 8, 4], F32)
        wsi = [0]
        xmi = [0]

        def bc(prm_idx, Tt, neg=False):
            src = NPR if neg else PRM
            return src[:, prm_idx, :].unsqueeze(2).to_broadcast([128, 8, Tt])

        def sigm_inplace(Ft, Tt):
            p.op("dve", lambda: nc.vector.tensor_scalar(out=Ft[:, :, 0:Tt], in0=Ft[:, :, 0:Tt], scalar1=1.0,
                                                        scalar2=None, op0=ALU.add), reads=[Ft], writes=[Ft])
            p.op("dve", lambda: nc.vector.reciprocal(out=Ft[:, :, 0:Tt], in_=Ft[:, :, 0:Tt]), reads=[Ft], writes=[Ft])

        def mixed(m, Tt):
            xm = XM[xmi[0] % 2]
            xmi[0] += 1
            for k in range(KC):
                p.op("dve", lambda: nc.vector.scalar_tensor_tensor(
                    out=xm[:, k, 0:Tt], in0=DX[:, k, 0:Tt], scalar=MIX[:, m, k:k + 1], in1=XT[:, k, 1:Tt + 1],
                    op0=ALU.mult, op1=ALU.add), reads=[DX, MIX, XT], writes=[xm])
            return xm

        def proj(xm, wname, Fout, Tt):
            for cc in range(8):
                wb = WS[wsi[0] % 3]
                wsi[0] += 1
                p.dma("sp", wb[:], io[wname][cc], writes=[wb])
                bk = bank()
                p.mm(bk[:, 0:Tt], [(wb[:, k, :], xm[:, k, 0:Tt]) for k in range(KC)], reads=[wb, xm], writes=[bk])
                p.op("act", lambda: nc.scalar.copy(out=Fout[:, cc, 0:Tt], in_=bk[:, 0:Tt]), reads=[bk], writes=[Fout])

        def lora_sig(xm, Wa, r, Wb_, nprm, Fout, Tt, tanh):
            bk = bank()
            p.mm(bk[0:r, 0:Tt], [(Wa[:, k, :], xm[:, k, 0:Tt]) for k in range(KC)], reads=[Wa, xm], writes=[bk])
            if tanh:
                p.op("act", lambda: nc.scalar.activation(out=LH[0:r, 0:Tt], in_=bk[0:r, 0:Tt], func=AF.Exp, scale=-2.0),
                     reads=[bk], writes=[LH])
                p.op("dve", lambda: nc.vector.tensor_scalar(out=LH[0:r, 0:Tt], in0=LH[0:r, 0:Tt], scalar1=1.0,
                                                            scalar2=None, op0=ALU.add), reads=[LH], writes=[LH])
                p.op("dve", lambda: nc.vector.reciprocal(out=LH[0:r, 0:Tt], in_=LH[0:r, 0:Tt]), reads=[LH], writes=[LH])
                p.op("dve", lambda: nc.vector.tensor_scalar(out=LHb[0:r, 0, 0:Tt], in0=LH[0:r, 0:Tt], scalar1=2.0,
                                                            scalar2=-1.0, op0=ALU.mult, op1=ALU.add),
                     reads=[LH], writes=[LHb])
            else:
                p.op("act", lambda: nc.scalar.copy(out=LHb[0:r, 0, 0:Tt], in_=bk[0:r, 0:Tt]), reads=[bk], writes=[LHb])
            for cc in range(8):
                bk = bank()
                p.mm(bk[:, 0:Tt], [(Wb_[0:r, cc * 128:(cc + 1) * 128], LHb[0:r, 0, 0:Tt])], reads=[Wb_, LHb], writes=[bk])
                p.op("act", lambda: nc.scalar.activation(out=Fout[:, cc, 0:Tt], in_=bk[:, 0:Tt], func=AF.Exp,
                                                         scale=-1.0, bias=NPR[:, nprm, cc:cc + 1]),
                     reads=[bk, NPR], writes=[Fout])

        F0, F1, F2, F3, F4, F5, F6, F7 = Fz
        for ti, (t0, chunks) in enumerate(rw_tiles(T)):
            Tt = sum(chunks)
            c0 = PADL + t0
            if t0 == 0:
                p.op("pool", lambda: nc.gpsimd.memset(XT[:, :, 0:1], 0.0), writes=[XT])
                p.dma("pool", XT[:, :, 1:Tt + 1], io["hn"][:, :, 0:Tt], writes=[XT])
            else:
                p.dma("pool", XT[:, :, 0:Tt + 1], io["hn"][:, :, t0 - 1:t0 + Tt], writes=[XT])
            p.op("dve", lambda: nc.vector.tensor_tensor(out=DX[:, :, 0:Tt], in0=XT[:, :, 0:Tt], in1=XT[:, :, 1:Tt + 1],
                                                        op=ALU.subtract), reads=[XT], writes=[DX])
            xm = mixed(0, Tt); proj(xm, "wr", F0, Tt)
            xm = mixed(2, Tt); proj(xm, "wk", F1, Tt)
            xm = mixed(3, Tt); proj(xm, "wv", F2, Tt)
            if layer2:
                lora_sig(xm, V1, 64, V2, 2, F3, Tt, False)
                sigm_inplace(F3, Tt)
                p.dma("pool", F4[:, :, 0:Tt], io["vf"][:, :, t0:t0 + Tt], writes=[F4])
                p.op("dve", lambda: nc.vector.tensor_tensor(out=F4[:, :, 0:Tt], in0=F4[:, :, 0:Tt], in1=F2[:, :, 0:Tt],
                                                            op=ALU.subtract), reads=[F4, F2], writes=[F4])
                p.op("dve", lambda: nc.vector.tensor_tensor(out=F4[:, :, 0:Tt], in0=F4[:, :, 0:Tt], in1=F3[:, :, 0:Tt],
                                                            op=ALU.mult), reads=[F4, F3], writes=[F4])
                p.op("dve", lambda: nc.vector.tensor_tensor(out=F2[:, :, 0:Tt], in0=F2[:, :, 0:Tt], in1=F4[:, :, 0:Tt],
                                                            op=ALU.add), reads=[F4, F2], writes=[F2])
            else:
                p.dma("pool", io["vf"][:, :, t0:t0 + Tt], F2[:, :, 0:Tt], reads=[F2], writes=[io["vf"].sub(ti)])
            p.op("act", lambda: nc.scalar.copy(out=VTt[:, :, 0:Tt], in_=F2[:, :, 0:Tt]), reads=[F2], writes=[VTt])
            xm = mixed(1, Tt); lora_sig(xm, W1, 96, W2, 0, F3, Tt, True); sigm_inplace(F3, Tt)
            xm = mixed(4, Tt); lora_sig(xm, A1, 96, A2, 1, F4, Tt, False); sigm_inplace(F4, Tt)
            xm = mixed(5, Tt)
            for j in range(2):
                bk = bank()
                p.mm(bk[:, 0:Tt], [(G1[:, k, j * 128:(j + 1) * 128], xm[:, k, 0:Tt]) for k in range(KC)],
                     reads=[G1, xm], writes=[bk])
                p.op("act", lambda: nc.scalar.activation(out=LH[:, 0:Tt], in_=bk[:, 0:Tt], func=AF.Exp, scale=-1.0),
                     reads=[bk], writes=[LH])
                p.op("dve", lambda: nc.vector.tensor_scalar(out=LH[:, 0:Tt], in0=LH[:, 0:Tt], scalar1=1.0, scalar2=None,
                                                            op0=ALU.add), reads=[LH], writes=[LH])
                p.op("dve", lambda: nc.vector.reciprocal(out=LH[:, 0:Tt], in_=LH[:, 0:Tt]), reads=[LH], writes=[LH])
                p.op("act", lambda: nc.scalar.copy(out=LHb[:, j, 0:Tt], in_=LH[:, 0:Tt]), reads=[LH], writes=[LHb])
            for cc in range(8):
                bk = bank()
                p.mm(bk[:, 0:Tt], [(G2[:, j, cc * 128:(cc + 1) * 128], LHb[:, j, 0:Tt]) for j in range(2)],
                     reads=[G2, LHb], writes=[bk])
                p.op("act", lambda: nc.scalar.copy(out=BGt[:, cc, 1, 0:Tt], in_=bk[:, 0:Tt]), reads=[bk], writes=[BGt])
            p.op("dve", lambda: nc.vector.tensor_tensor(out=F5[:, :, 0:Tt], in0=F1[:, :, 0:Tt], in1=bc(3, Tt), op=ALU.mult),
                 reads=[F1, PRM], writes=[F5])
            p.op("act", lambda: nc.scalar.activation(out=SQb[:, :, 0:Tt], in_=F5[:, :, 0:Tt], func=AF.Square),
                 reads=[F5], writes=[SQb])
            for cc in range(8):
                bk = bank()
                p.mm(bk[:, 0:Tt], [(BLK[:], SQb[:, cc, 0:Tt])], reads=[BLK, SQb], writes=[bk])
                p.op("act", lambda: nc.scalar.activation(out=F6[:, cc, 0:Tt], in_=bk[:, 0:Tt], func=AF.Sqrt),
                     reads=[bk], writes=[F6])
            p.op("dve", lambda: nc.vector.tensor_scalar(out=F6[:, :, 0:Tt], in0=F6[:, :, 0:Tt], scalar1=1e-12, scalar2=None,
                                                        op0=ALU.max), reads=[F6], writes=[F6])
            p.op("dve", lambda: nc.vector.reciprocal(out=F6[:, :, 0:Tt], in_=F6[:, :, 0:Tt]), reads=[F6], writes=[F6])
            p.op("dve", lambda: nc.vector.tensor_tensor(out=F5[:, :, 0:Tt], in0=F5[:, :, 0:Tt], in1=F6[:, :, 0:Tt], op=ALU.mult),
                 reads=[F5, F6], writes=[F5])
            p.op("dve", lambda: nc.vector.scalar_tensor_tensor(out=F6[:, :, 0:Tt], in0=F4[:, :, 0:Tt], scalar=-1.0,
                                                               in1=bc(4, Tt), op0=ALU.add, op1=ALU.mult),
                 reads=[F4, PRM], writes=[F6])
            p.op("dve", lambda: nc.vector.scalar_tensor_tensor(out=F1[:, :, 0:Tt], in0=F6[:, :, 0:Tt], scalar=1.0,
                                                               in1=F1[:, :, 0:Tt], op0=ALU.add, op1=ALU.mult),
                 reads=[F6, F1], writes=[F1])
            p.op("dve", lambda: nc.vector.tensor_tensor(out=F6[:, :, 0:Tt], in0=F5[:, :, 0:Tt], in1=F4[:, :, 0:Tt], op=ALU.mult),
                 reads=[F5, F4], writes=[F6])
            for cc in range(8):
                p.op("dve", lambda: nc.vector.tensor_tensor_scan(out=F4[:, cc, 0:Tt], data0=rm[:, t0:t0 + Tt],
                                                                 data1=F3[:, cc, 0:Tt], initial=0.0,
                                                                 op0=ALU.mult, op1=ALU.add),
                     reads=[rm, F3], writes=[F4])
            p.op("dve", lambda: nc.vector.tensor_tensor(out=F7[:, :, 0:Tt], in0=F4[:, :, 0:Tt], in1=F3[:, :, 0:Tt],
                                                        op=ALU.subtract), reads=[F4, F3], writes=[F7])
            p.op("act", lambda: nc.scalar.activation(out=F7[:, :, 0:Tt], in_=F7[:, :, 0:Tt], func=AF.Exp, scale=-CW),
                 reads=[F7], writes=[F7])
            p.op("dve", lambda: nc.vector.scalar_tensor_tensor(out=ARt[:, :, 0, 0:Tt], in0=F5[:, :, 0:Tt], scalar=-1.0,
                                                               in1=F7[:, :, 0:Tt], op0=ALU.mult, op1=ALU.mult),
                 reads=[F5, F7], writes=[ARt])
            p.op("act", lambda: nc.scalar.activation(out=F3[:, :, 0:Tt], in_=F4[:, :, 0:Tt], func=AF.Exp, scale=-CW),
                 reads=[F4], writes=[F3])
            p.op("dve", lambda: nc.vector.tensor_tensor(out=ARt[:, :, 1, 0:Tt], in0=F0[:, :, 0:Tt], in1=F3[:, :, 0:Tt],
                                                        op=ALU.mult), reads=[F0, F3], writes=[ARt])
            p.op("act", lambda: nc.scalar.activation(out=F3[:, :, 0:Tt], in_=F4[:, :, 0:Tt], func=AF.Exp, scale=CW),
                 reads=[F4], writes=[F3])
            p.op("dve", lambda: nc.vector.tensor_tensor(out=BKt[:, :, 0, 0:Tt], in0=F6[:, :, 0:Tt], in1=F3[:, :, 0:Tt],
                                                        op=ALU.mult), reads=[F6, F3], writes=[BKt])
            p.op("dve", lambda: nc.vector.tensor_tensor(out=BKt[:, :, 1, 0:Tt], in0=F1[:, :, 0:Tt], in1=F3[:, :, 0:Tt],
                                                        op=ALU.mult), reads=[F1, F3], writes=[BKt])
            off = 0
            for j, C in enumerate(chunks):
                p.op("act", lambda: nc.scalar.activation(out=PCt[:, :, j:j + 1], in_=F4[:, :, off + C - 1:off + C],
                                                         func=AF.Exp, scale=-CW), reads=[F4], writes=[PCt])
                p.op("dve", lambda: nc.vector.tensor_tensor(
                    out=F3[:, :, off:off + C], in0=F4[:, :, off:off + C],
                    in1=F4[:, :, off + C - 1:off + C].to_broadcast([128, 8, C]), op=ALU.subtract),
                    reads=[F4], writes=[F3])
                off += C
            p.op("act", lambda: nc.scalar.activation(out=F3[:, :, 0:Tt], in_=F3[:, :, 0:Tt], func=AF.Exp, scale=CW),
                 reads=[F3], writes=[F3])
            p.op("dve", lambda: nc.vector.tensor_tensor(out=BKBt[:, :, 0, 0:Tt], in0=F6[:, :, 0:Tt], in1=F3[:, :, 0:Tt],
                                                        op=ALU.mult), reads=[F6, F3], writes=[BKBt])
            p.op("dve", lambda: nc.vector.tensor_tensor(out=BKBt[:, :, 1, 0:Tt], in0=F1[:, :, 0:Tt], in1=F3[:, :, 0:Tt],
                                                        op=ALU.mult), reads=[F1, F3], writes=[BKBt])
            p.op("dve", lambda: nc.vector.tensor_tensor(out=F5[:, :, 0:Tt], in0=F0[:, :, 0:Tt], in1=F1[:, :, 0:Tt], op=ALU.mult),
                 reads=[F0, F1], writes=[F5])
            p.op("dve", lambda: nc.vector.tensor_tensor(out=SQb[:, :, 0:Tt], in0=F5[:, :, 0:Tt], in1=bc(5, Tt), op=ALU.mult),
                 reads=[F5, PRM], writes=[SQb])
            for cc in range(8):
                bk = bank()
                p.mm(bk[:, 0:Tt], [(BLK[:], SQb[:, cc, 0:Tt])], reads=[BLK, SQb], writes=[bk])
                p.op("dve", lambda: nc.vector.tensor_tensor(out=BGt[:, cc, 0, 0:Tt], in0=bk[:, 0:Tt], in1=F2[:, cc, 0:Tt],
                                                            op=ALU.mult), reads=[bk, F2], writes=[BGt])
            c0i = c0 // 64 if t0 > 0 else 0
            for cc in range(8):
                for nm, tl in (("ARs", ARt), ("BKs", BKt), ("BKBs", BKBt), ("BGs", BGt)):
                    p.dma("pool", io[nm][cc, :, :, c0:c0 + Tt], tl[:, cc, :, 0:Tt], reads=[tl],
                          writes=[io[nm].sub((ti, cc))])
                p.dma("pool", io["VTs"][cc, :, 64 + c0:64 + c0 + Tt], VTt[:, cc, 0:Tt], reads=[VTt],
                      writes=[io["VTs"].sub((ti, cc))])
            p.dma("pool", io["PCs"][:, :, c0i:c0i + len(chunks)].rearrange("c p n -> p c n"), PCt[:, :, 0:len(chunks)],
                  reads=[PCt], writes=[io["PCs"].sub(ti)], allow_slow_non_contiguous=True)
        barrier(p)
    import os
    if os.environ.get("RW_STOP") == "A":
        p.es = es_outer
        return
    with ExitStack() as esB:
        p.es = esB
        NS = 3
        BLKC = 8
        blocks = []
        c = 0
        while c < NCH:
            n = min(BLKC, NCH - c)
            blocks.append((c, n))
            c += n
        MASK2 = p.sb("MASK2", [128, 128], F32)
        MASKL = p.sb("MASKL", [64, 64], F32)
        ID = p.sb("ID", [128, 128], BF16)
        O64 = p.sb("O64", [64, 64], F32)
        PRM = p.sb("PRMB", [128, 8, 8], F32)
        for t_, k_ in ((MASK2, "mask2"), (MASKL, "maskl"), (ID, "ident"), (O64, "o64"), (PRM, "prm")):
            p.dma("sp", t_[:], io[k_][:], writes=[t_])
        TB = BLKC * 64
        TRB = cx.trb
        ebi = [0]

        def bank_e():
            b = banks[6]
            ebi[0] += 1
            return b
        sl = []
        for s in range(NS):
            d = {}
            d["AR"] = [p.sb(f"AR{s}{i}", [64, 2 * TB], BF16) for i in range(2)]
            d["BK"] = [p.sb(f"BK{s}{i}", [64, 2 * TB], BF16) for i in range(2)]
            d["BKB"] = [p.sb(f"BKB{s}{i}", [64, 2 * TB], BF16) for i in range(2)]
            d["BG"] = [p.sb(f"BG{s}{i}", [64, 2, TB], BF16) for i in range(2)]
            d["VT"] = [p.sb(f"VT{s}{i}", [64, 64 + TB], BF16) for i in range(2)]
            d["PC"] = p.sb(f"PC{s}", [64, NCH], F32)
            d["LNP"] = p.sb(f"LNP{s}", [64, 2], F32)
            d["Hs"] = p.sb(f"Hs{s}", [64, 64], F32)
            d["Hb"] = p.sb(f"Hb{s}", [64, 64], BF16)
            d["GM"] = p.sb(f"GM{s}", [128, 128], BF16)
            d["AJ"] = [p.sb(f"AJ{s}{j}", [64, 2, 64], BF16) for j in range(6)]
            d["UV"] = p.sb(f"UV{s}", [128, 64], BF16)
            d["BKBt"] = p.sb(f"BKBt{s}", [128, 64], BF16)
            d["Xb"] = p.sb(f"Xb{s}", [64, 64], BF16)
            d["Xf"] = p.sb(f"Xf{s}", [64, 64], F32)
            d["YT"] = p.sb(f"YT{s}", [64, TB], F32)
            d["Dd"] = p.sb(f"Dd{s}", [64, TB], F32)
            d["Sq"] = p.sb(f"Sq{s}", [64, TB], F32)
            d["RSd"] = p.sb(f"RSd{s}", [64, TB], F32)
            d["OUT"] = p.sb(f"OUT{s}", [64, TB], BF16)
            d["pm"] = banks[2 * s]
            d["px"] = banks[2 * s + 1]
            sl.append(d)
        LVL = int(os.environ.get("RW_B", "9"))
        for grp in range((16 + NS - 1) // NS):
            heads = [h for h in range(grp * NS, min(16, grp * NS + NS))]
            for s, h in enumerate(heads):
                d = sl[s]
                cc, hf = h // 2, h % 2
                prt = slice(hf * 64, (hf + 1) * 64)
                p.dma("sp", d["PC"][:], io["PCs"][cc, prt, :], reads=[io["PCs"].sub(t) for t in range(len(rw_tiles(T)))],
                      writes=[d["PC"]])
                p.dma("sp", d["LNP"][:], io["prm"][prt, 6:8, cc], writes=[d["LNP"]], allow_slow_non_contiguous=True)
                p.op("pool", lambda: nc.gpsimd.memset(d["Hs"][:], 0.0), writes=[d["Hs"]])
                p.op("pool", lambda: nc.gpsimd.memset(d["Hb"][:], 0.0), writes=[d["Hb"]])
            for bidx, (cb, nb) in enumerate(blocks):
                Tb = nb * 64
                col0 = cb * 64
                u = bidx % 2
                for s, h in enumerate(heads):
                    d = sl[s]
                    cc, hf = h // 2, h % 2
                    prt = slice(hf * 64, (hf + 1) * 64)
                    alls = lambda nm: list(io[nm].subs.values())
                    for nm, key in (("ARs", "AR"), ("BKs", "BK"), ("BKBs", "BKB")):
                        for a_ in range(2):
                            p.dma("sp", d[key][u][:, 0:2 * Tb].rearrange("p (c a t) -> p c a t", a=2, t=64)[:, :, a_, :],
                                  io[nm][cc, prt, a_, col0:col0 + Tb].rearrange("p (c t) -> p c t", t=64), reads=alls(nm),
                                  writes=[d[key][u]])
                    p.dma("sp", d["BG"][u][:, :, 0:Tb], io["BGs"][cc, prt, :, col0:col0 + Tb], reads=alls("BGs"),
                          writes=[d["BG"][u]])
                    p.dma("sp", d["VT"][u][:, 0:64 + Tb], io["VTs"][cc, prt, col0:col0 + 64 + Tb], reads=alls("VTs"),
                          writes=[d["VT"][u]])
                for ci in range(nb if LVL >= 2 else 0):
                    cs = slice(ci * 64, (ci + 1) * 64)
                    c2 = slice(ci * 128, (ci + 1) * 128)
                    ca = slice(ci * 128, ci * 128 + 64)
                    cr = slice(ci * 128 + 64, ci * 128 + 128)
                    cg = cb + ci
                    for s, h in enumerate(heads):
                        d = sl[s]
                        AR, BK, BKB, VT = d["AR"][u], d["BK"][u], d["BKB"][u], d["VT"][u]
                        pm = d["pm"]
                        GM, AJ = d["GM"], d["AJ"]
                        p.mm(pm[:, 0:128], [(BK[:, c2], AR[:, c2])], reads=[BK, AR], writes=[pm])
                        p.op("dve", lambda: nc.vector.tensor_tensor(out=GM[:], in0=pm[:, 0:128], in1=MASK2[:], op=ALU.mult),
                             reads=[pm, MASK2], writes=[GM])
                        L2 = int(os.environ.get("RW_B2", "9"))
                        if L2 < 2:
                            continue
                        p.mm(pm[0:64, 128:192], [(AR[:, ca], BK[:, ca])], reads=[BK, AR], writes=[pm])
                        if os.environ.get("SKIP_A") != "1":
                            p.op("dve", lambda: nc.vector.tensor_copy(out=AJ[0][:, 0, :], in_=GM[0:64, 0:64]), reads=[GM], writes=[AJ[0].sub(0) if os.environ.get("SEPW") == "1" else AJ[0]])
                        if os.environ.get("SKIP_D") == "1":
                            continue
                        p.op("dve", lambda: nc.vector.tensor_tensor(out=AJ[0][:, 1, :], in0=pm[0:64, 128:192], in1=MASKL[:],
                                                                    op=ALU.mult), reads=[pm, MASKL], writes=[AJ[0]])
                        if L2 < 3:
                            continue
                        trv = TRB[:, s * 128:s * 128 + 64]
                        trk = TRB[:, s * 128 + 64:s * 128 + 128]
                        trreg = TRB

                        def trs():
                            nc.tensor.transpose(trv, VT[:, ci * 64:ci * 64 + 128], ID[0:64, 0:64])
                            return nc.tensor.transpose(trk, BKB[:, c2], ID[0:64, 0:64])
                        p.pe_multi(trs, reads=[VT, BKB, ID], writes=[trreg])
                        if L2 < 4:
                            continue
                        p.op("dve", lambda: nc.vector.tensor_copy(out=d["UV"][64:128, :], in_=trv[64:128, :]),
                             reads=[trreg], writes=[d["UV"].sub("v")])
                        p.op("dve", lambda: nc.vector.tensor_copy(out=d["BKBt"][:], in_=trk),
                             reads=[trreg], writes=[d["BKBt"]])
                    for j in range(5 if LVL >= 3 else 0):
                        for s, h in enumerate(heads):
                            d = sl[s]
                            pm, AJ = d["pm"], d["AJ"]
                            reg = pm
                            o = 192
                            def sq():
                                nc.tensor.matmul(pm[0:64, o:o + 64], lhsT=AJ[j][:, 1, :], rhs=AJ[j][:, 0, :], start=True, stop=True)
                                return nc.tensor.matmul(pm[0:64, o + 64:o + 128], lhsT=AJ[j][:, 0, :], rhs=AJ[j][:, 1, :],
                                                        start=True, stop=True)
                            p.pe_multi(sq, reads=[AJ[j]], writes=[reg])
                            eng = "dve"
                            if eng == "act":
                                p.op("dve", lambda: nc.vector.tensor_copy(out=AJ[j + 1][:, :, :],
                                                                   in_=pm[0:64, o:o + 128].rearrange("p (a b) -> p a b", b=64)),
                                     reads=[reg], writes=[AJ[j + 1]])
                            else:
                                p.op("dve", lambda: nc.vector.tensor_copy(out=AJ[j + 1][:, :, :],
                                                                          in_=pm[0:64, o:o + 128].rearrange("p (a b) -> p a b", b=64)),
                                     reads=[reg], writes=[AJ[j + 1]])
                    if LVL < 4:
                        continue
                    for s, h in enumerate(heads):
                        d = sl[s]
                        AR = d["AR"][u]
                        px = d["px"]
                        prs = [(AR[:, ca], d["Hb"][:]), (d["GM"][64:128, 0:64], d["UV"][64:128, :])]
                        if os.environ.get("W0_ONE") == "1":
                            prs = prs[:1]
                        if os.environ.get("W0_ONE") == "2":
                            prs = prs[1:]
                        ev0 = p.mm(px[0:64, 0:64], prs[:1], reads=[AR, d["Hb"]], writes=[px], stop=False)
                        p.mm(px[0:64, 0:64], prs[1:], reads=[d["GM"], d["UV"].sub("v")], writes=[px], start=False, after=ev0)
                        p.op("dve", lambda: nc.vector.tensor_copy(out=d["Xf"][:], in_=px[0:64, 0:64]), reads=[px], writes=[d["Xf"]])
                        p.op("act", lambda: nc.scalar.copy(out=d["Xb"][:], in_=d["Xf"][:]), reads=[d["Xf"]], writes=[d["Xb"]])
                    for j in range(6 if LVL >= 5 else 0):
                        for s, h in enumerate(heads):
                            d = sl[s]
                            px = d["px"]
                            p.mm(px[0:64, 0:64], [(d["AJ"][j][:, 0, :], d["Xb"][:])], reads=[d["AJ"][j], d["Xb"]], writes=[px])
                            p.op("dve", lambda: nc.vector.tensor_tensor(out=d["Xf"][:], in0=d["Xf"][:], in1=px[0:64, 0:64], op=ALU.add),
                                 reads=[px, d["Xf"]], writes=[d["Xf"]])
                            if j < 5:
                                p.op("act", lambda: nc.scalar.copy(out=d["Xb"][:], in_=d["Xf"][:]), reads=[d["Xf"]], writes=[d["Xb"]])
                            else:
                                p.op("act", lambda: nc.scalar.copy(out=d["UV"][0:64, :], in_=d["Xf"][:]), reads=[d["Xf"]],
                                     writes=[d["UV"].sub("u")])
                    for s, h in enumerate(heads if LVL >= 6 else []):
                        d = sl[s]
                        AR = d["AR"][u]
                        pm = d["pm"]
                        px = d["px"]
                        uvr = [d["UV"].sub("u"), d["UV"].sub("v")]
                        ev0 = p.mm(px[0:64, 128:192], [(d["Hb"][:], AR[:, cr])], reads=[d["Hb"], AR], writes=[px], stop=False)
                        p.mm(px[0:64, 128:192], [(d["UV"][:], d["GM"][:, 64:128])],
                             reads=[d["GM"]] + uvr, writes=[px], start=False, after=ev0)
                        p.op("dve", lambda: nc.vector.tensor_copy(out=d["YT"][:, cs], in_=px[0:64, 128:192]), reads=[px],
                             writes=[d["YT"]])
                        p.mm(px[0:64, 64:128], [(d["BKBt"][:], d["UV"][:])], reads=[d["BKBt"]] + uvr, writes=[px])
                        p.op("dve", lambda: nc.vector.scalar_tensor_tensor(out=d["Hs"][:], in0=d["Hs"][:],
                                                                           scalar=d["PC"][:, cg:cg + 1], in1=px[0:64, 64:128],
                                                                           op0=ALU.mult, op1=ALU.add),
                             reads=[d["Hs"], d["PC"], px], writes=[d["Hs"]])
                        p.op("dve", lambda: nc.vector.tensor_copy(out=d["Hb"][:], in_=d["Hs"][:]), reads=[d["Hs"]], writes=[d["Hb"]])
                for s, h in enumerate(heads if LVL >= 7 else []):
                    d = sl[s]
                    cc, hf = h // 2, h % 2
                    prt = slice(hf * 64, (hf + 1) * 64)
                    pm = d["pm"]
                    YT, Dd, Sq, RSd, OUT, BG = d["YT"], d["Dd"], d["Sq"], d["RSd"], d["OUT"], d["BG"][u]
                    bk = bank_e()
                    p.mm(bk[0:64, 0:Tb], [(O64[:], YT[:, 0:Tb])], reads=[O64, YT], writes=[bk])
                    p.op("dve", lambda: nc.vector.tensor_tensor(out=Dd[:, 0:Tb], in0=YT[:, 0:Tb], in1=bk[0:64, 0:Tb],
                                                                op=ALU.subtract), reads=[YT, bk], writes=[Dd])
                    p.op("act", lambda: nc.scalar.activation(out=Sq[:, 0:Tb], in_=Dd[:, 0:Tb], func=AF.Square),
                         reads=[Dd], writes=[Sq])
                    bk = bank_e()
                    p.mm(bk[0:64, 0:Tb], [(O64[:], Sq[:, 0:Tb])], reads=[O64, Sq], writes=[bk])
                    p.op("act", lambda: nc.scalar.activation(out=RSd[:, 0:Tb], in_=bk[0:64, 0:Tb], func=AF.Ln, bias=64e-5),
                         reads=[bk], writes=[RSd])
                    p.op("act", lambda: nc.scalar.activation(out=RSd[:, 0:Tb], in_=RSd[:, 0:Tb], func=AF.Exp, scale=-0.5),
                         reads=[RSd], writes=[RSd])
                    p.op("dve", lambda: nc.vector.tensor_tensor(out=Dd[:, 0:Tb], in0=Dd[:, 0:Tb], in1=RSd[:, 0:Tb], op=ALU.mult),
                         reads=[Dd, RSd], writes=[Dd])
                    p.op("dve", lambda: nc.vector.tensor_scalar(out=Dd[:, 0:Tb], in0=Dd[:, 0:Tb], scalar1=d["LNP"][:, 0:1],
                                                                scalar2=d["LNP"][:, 1:2], op0=ALU.mult, op1=ALU.add),
                         reads=[Dd, d["LNP"]], writes=[Dd])
                    p.op("dve", lambda: nc.vector.tensor_tensor(out=Dd[:, 0:Tb], in0=Dd[:, 0:Tb], in1=BG[:, 0, 0:Tb], op=ALU.add),
                         reads=[Dd, BG], writes=[Dd])
                    p.op("dve", lambda: nc.vector.tensor_tensor(out=OUT[:, 0:Tb], in0=Dd[:, 0:Tb], in1=BG[:, 1, 0:Tb], op=ALU.mult),
                         reads=[Dd, BG], writes=[OUT])
                    lo = max(col0, PADL)
                    p.dma("pool", io["yg"][prt, cc, lo - PADL:col0 + Tb - PADL], OUT[:, lo - col0:Tb], reads=[OUT],
                          writes=[io["yg"].sub((h, bidx))])
        barrier(p)
    p.es = es_outer
import ml_dtypes
_bf = ml_dtypes.bfloat16

D_MODEL = 2048
NB = 4
SEQ = 4096
NMETA = 16
TSEQ = SEQ + NMETA
NTC = TSEQ // 2
DFF = 8192
NCORE = 8
CAST_W = 8192


def _fm(a):
    return np.ascontiguousarray(a.reshape(-1, 128, a.shape[1]).transpose(1, 0, 2))


def _unfm(a):
    return a.transpose(1, 0, 2).reshape(-1, a.shape[2])


def _wl(w):
    K, M = w.shape
    return np.ascontiguousarray(w.reshape(K // 128, 128, M // 128, 128).transpose(2, 1, 0, 3))


def _wr(w):
    K, M = w.shape
    return np.ascontiguousarray(w.reshape(K // 128, 128, M).transpose(1, 0, 2))


def _vec(v):
    return np.ascontiguousarray(v.reshape(-1, 128).T)


_PROGS = {}


def _prog(key, builder):
    if key not in _PROGS:
        _PROGS[key] = builder()
    return _PROGS[key]


def _new_nc():
    return bass.Bass("TRN2", target_bir_lowering=False)


def build_cast(ncols):
    nc = _new_nc()
    with ExitStack() as es:
        p = Prog(nc, es)
        src = p.dram("src", [128, ncols], F32, kind="ExternalInput")
        dst = p.dram("dst", [128, ncols], BF16, kind="ExternalOutput")
        IN = [p.sb(f"cin{i}", [128, CAST_W], F32) for i in range(3)]
        OUT = [p.sb(f"cout{i}", [128, CAST_W], BF16) for i in range(3)]
        nt = ncols // CAST_W
        for t in range(nt):
            a, b = IN[t % 3], OUT[t % 3]
            sl = slice(t * CAST_W, (t + 1) * CAST_W)
            p.dma("sp", a[:], src[:, sl], writes=[a])
            h = CAST_W // 2
            p.op("dve", lambda: nc.vector.tensor_copy(out=b[:, 0:h], in_=a[:, 0:h]), reads=[a], writes=[b.sub(0)])
            p.op("act", lambda: nc.scalar.copy(out=b[:, h:], in_=a[:, h:]), reads=[a], writes=[b.sub(1)])
            p.dma("pool", dst[:, sl], b[:], reads=[b.sub(0), b.sub(1)], writes=[dst.sub(t)])
        p.finish([dst], q="pool")
    return nc


def build_norm0():
    nc = _new_nc()
    with ExitStack() as es:
        p = Prog(nc, es)
        io = {"h_in": p.dram("h_in", [128, 16, NTC], F32, kind="ExternalInput"),
              "g_next": p.dram("g_next", [128, 16], F32, kind="ExternalInput"),
              "hn_out": p.dram("hn_out", [128, 16, NTC], BF16, kind="ExternalOutput")}
        cx = Ctx(p, nc)
        stage_norm(cx, io, NTC, D_MODEL)
        p.finish([io["hn_out"]], q="pool")
    return nc


def build_s3(last):
    nc = _new_nc()
    KC, FC = 16, 64
    with ExitStack() as es:
        p = Prog(nc, es)
        io = {}
        io["yg"] = p.dram("yg", [128, KC, NTC], BF16, kind="ExternalInput")
        io["h_in"] = p.dram("h_in", [128, KC, NTC], F32, kind="ExternalInput")
        io["wo"] = p.dram("wo", [KC, 128, KC, 128], BF16, kind="ExternalInput")
        io["w1"] = p.dram("w1", [FC, 128, KC, 128], BF16, kind="ExternalInput")
        io["w2"] = p.dram("w2", [KC, 128, FC, 128], BF16, kind="ExternalInput")
        io["g_mlp"] = p.dram("g_mlp", [128, KC], F32, kind="ExternalInput")
        io["g_next"] = p.dram("g_next", [128, KC], F32, kind="ExternalInput")
        if not last:
            io["h_out"] = p.dram("h_out", [128, KC, NTC], F32, kind="ExternalOutput")
            io["hn_out"] = p.dram("hn_out", [128, KC, NTC], BF16, kind="ExternalOutput")
            outs = [io["h_out"], io["hn_out"]]
        else:
            io["out"] = p.dram("out", [128, KC, NTC], F32, kind="ExternalOutput")
            outs = [io["out"]]
        cx = Ctx(p, nc)
        stage3(cx, io, NTC, D_MODEL, DFF, last)
        p.finish(outs, q="pool")
    return nc


def build_gla():
    nc = _new_nc()
    T = TSEQ
    with ExitStack() as es:
        p = Prog(nc, es)
        io = {}

        def inp(n, sh, dt):
            io[n] = p.dram(n, sh, dt, kind="ExternalInput")
        inp("hn", [128, 16, T], BF16); inp("wq", [128, 16, 512], BF16); inp("wk", [128, 16, 512], BF16)
        inp("wv", [128, 16, 1024], BF16); inp("wg", [128, 16, 1024], BF16); inp("wza", [128, 16, 16], BF16)
        inp("wa2", [16, 512], BF16); inp("ba", [128, 4], F32); inp("gnw", [1, 1024], F32)
        inp("mask_incl", [64, 64], F32); inp("ident", [128, 128], BF16)
        io["og"] = p.dram("og", [128, 8, T], BF16, kind="ExternalOutput")
        cx = Ctx(p, nc, nbanks=0)
        stage_gla(cx, io, T)
        p.finish([io["og"]], q="pool")
    return nc


def build_rw(layer2):
    nc = _new_nc()
    T = TSEQ
    TP = PADL + T
    NCH = TP // 64
    with ExitStack() as es:
        p = Prog(nc, es)
        io = {}

        def inp(n, sh, dt):
            io[n] = p.dram(n, sh, dt, kind="ExternalInput")
        inp("hn", [128, 16, T], BF16)
        for n in ("wr", "wk", "wv"):
            inp(n, [8, 128, 16, 128], BF16)
        inp("w1", [128, 16, 96], BF16); inp("a1", [128, 16, 96], BF16); inp("w2", [96, 1024], BF16); inp("a2", [96, 1024], BF16)
        inp("g1", [128, 16, 256], BF16); inp("g2", [128, 2, 1024], BF16); inp("mix", [128, 6, 16], F32); inp("prm", [128, 8, 8], F32)
        inp("blk", [128, 128], BF16); inp("mask2", [128, 128], F32); inp("maskl", [64, 64], F32)
        inp("ident", [128, 128], BF16); inp("o64", [64, 64], F32)
        if layer2:
            inp("v1", [128, 16, 64], BF16); inp("v2", [64, 1024], BF16); inp("vf", [128, 8, T], F32)
        else:
            io["vf"] = p.dram("vf", [128, 8, T], F32, kind="ExternalOutput")
        for n in ("ARs", "BKs", "BKBs", "BGs"):
            io[n] = p.dram(n, [8, 128, 2, TP], BF16)
        io["VTs"] = p.dram("VTs", [8, 128, 64 + TP], BF16)
        io["PCs"] = p.dram("PCs", [8, 128, NCH], F32)
        io["yg"] = p.dram("yg", [128, 8, T], BF16, kind="ExternalOutput")
        cx = Ctx(p, nc, nbanks=7)
        cx.trb = p.ps("trb", [128, 1024], BF16)
        stage_rw(cx, io, T, layer2)
        p.finish([io["yg"]] + ([] if layer2 else [io["vf"]]), q="pool")
    return nc


def _run(nc, in_maps):
    res = run_bass_kernel_spmd(nc, in_maps, core_ids=list(range(NCORE)))
    return res.results


def _cast_all(ws):
    names = list(ws.keys())
    sizes = [ws[n].size for n in names]
    total = sum(sizes)
    unit = NCORE * 128 * CAST_W
    padded = ((total + unit - 1) // unit) * unit
    flat = np.zeros(padded, np.float32)
    o = 0
    for n in names:
        flat[o:o + ws[n].size] = ws[n].reshape(-1)
        o += ws[n].size
    ncols = padded // (NCORE * 128)
    flat = flat.reshape(NCORE, 128, ncols)
    nc = _prog(("cast", ncols), lambda: build_cast(ncols))
    res = _run(nc, [{"src": flat[c]} for c in range(NCORE)])
    out = np.concatenate([np.asarray(r["dst"]).reshape(-1) for r in res])
    d = {}
    o = 0
    for n in names:
        d[n] = out[o:o + ws[n].size].reshape(ws[n].shape)
        o += ws[n].size
    return d


def kernel(x, meta, norm_mix, norm_mlp, norm_f, mlp_w1, mlp_w2, rw_mix, rw_w_rkv, rw_w0, rw_w1,
           rw_w2, rw_a0, rw_a1, rw_a2, rw_v0, rw_v1, rw_v2, rw_g1, rw_g2, rw_k_k, rw_k_a, rw_r_k,
           rw_ln_w, rw_ln_b, rw_w_o, gla_w_in, gla_w_a2, gla_b_a, gla_gn_w, gla_w_o):
    f32 = np.float32
    A = lambda a: np.asarray(a, dtype=f32)
    x, meta = A(x), A(meta)
    wb = _cast_all({"mlp_w1": A(mlp_w1), "mlp_w2": A(mlp_w2), "rw_w_rkv": A(rw_w_rkv), "rw_w1": A(rw_w1),
                    "rw_w2": A(rw_w2), "rw_a1": A(rw_a1), "rw_a2": A(rw_a2), "rw_v1": A(rw_v1), "rw_v2": A(rw_v2),
                    "rw_g1": A(rw_g1), "rw_g2": A(rw_g2), "rw_w_o": A(rw_w_o), "gla_w_in": A(gla_w_in),
                    "gla_w_a2": A(gla_w_a2), "gla_w_o": A(gla_w_o)})
    norm_mix, norm_mlp, norm_f = A(norm_mix), A(norm_mlp), A(norm_f)
    ident = np.eye(128, dtype=f32).astype(_bf)
    su = np.triu(np.ones((64, 64), f32), 1)
    iu = np.triu(np.ones((64, 64), f32))
    mask2 = np.ascontiguousarray(np.block([[su, iu], [su, iu]]))
    maskl = np.tril(np.ones((64, 64), f32), -1)
    blk = np.zeros((128, 128), f32); blk[:64, :64] = 1; blk[64:, 64:] = 1
    blk = blk.astype(_bf)
    o64 = np.full((64, 64), 1.0 / 64, f32)
    h = []
    for c in range(NCORE):
        b, s = c // 2, c % 2
        seq = np.concatenate([meta, x[b]], axis=0)
        h.append(_fm(np.ascontiguousarray(seq[s * NTC:(s + 1) * NTC].T)))
    nc0 = _prog("norm0", build_norm0)
    res = _run(nc0, [{"h_in": h[c], "g_next": _vec(norm_mix[0])} for c in range(NCORE)])
    hn = [np.asarray(r["hn_out"]) for r in res]
    vfirst = [None] * NCORE
    for i in range(4):
        j = i // 2
        hn_full = [np.concatenate([hn[2 * b], hn[2 * b + 1]], axis=2) for b in range(NB)]
        in_maps = []
        if i % 2 == 0:
            layer2 = (j > 0)
            ncm = _prog(("rw", layer2), lambda: build_rw(layer2))
            for c in range(NCORE):
                b, s = c // 2, c % 2
                cs = slice(s * 1024, (s + 1) * 1024)
                v0 = A(rw_v0)[j - 1] if layer2 else np.zeros(D_MODEL, f32)
                prm = np.stack([A(rw_w0)[j][cs], A(rw_a0)[j][cs], v0[cs], A(rw_k_k)[j][cs], A(rw_k_a)[j][cs],
                                A(rw_r_k)[j].reshape(-1)[cs], A(rw_ln_w)[j][cs], A(rw_ln_b)[j][cs]], 0)
                prm = np.ascontiguousarray(prm.reshape(8, 8, 128).transpose(2, 0, 1))
                m = {"hn": hn_full[b],
                     "wr": _wl(wb["rw_w_rkv"][j, 0][:, cs]), "wk": _wl(wb["rw_w_rkv"][j, 1][:, cs]),
                     "wv": _wl(wb["rw_w_rkv"][j, 2][:, cs]),
                     "w1": _wr(wb["rw_w1"][j]), "a1": _wr(wb["rw_a1"][j]), "g1": _wr(wb["rw_g1"][j]),
                     "w2": np.ascontiguousarray(wb["rw_w2"][j][:, cs]), "a2": np.ascontiguousarray(wb["rw_a2"][j][:, cs]),
                     "g2": np.ascontiguousarray(wb["rw_g2"][j][:, cs].reshape(2, 128, 1024).transpose(1, 0, 2)),
                     "mix": np.ascontiguousarray(A(rw_mix)[j].reshape(6, 16, 128).transpose(2, 0, 1)),
                     "prm": prm, "blk": blk, "mask2": mask2, "maskl": maskl, "ident": ident, "o64": o64}
                if layer2:
                    m["v1"] = _wr(wb["rw_v1"][j - 1])
                    m["v2"] = np.ascontiguousarray(wb["rw_v2"][j - 1][:, cs])
                    m["vf"] = vfirst[c]
                in_maps.append(m)
            res = _run(ncm, in_maps)
            mix_out = [np.asarray(r["yg"]) for r in res]
            if not layer2:
                vfirst = [np.asarray(r["vf"]) for r in res]
            wo = wb["rw_w_o"][j]
        else:
            ncm = _prog("gla", build_gla)
            w_in = wb["gla_w_in"][j]
            for c in range(NCORE):
                b, s = c // 2, c % 2
                m = {"hn": hn_full[b],
                     "wq": _wr(w_in[:, s * 512:(s + 1) * 512]),
                     "wk": _wr(w_in[:, 1024 + s * 512:1024 + (s + 1) * 512]),
                     "wv": _wr(w_in[:, 2048 + s * 1024:2048 + (s + 1) * 1024]),
                     "wg": _wr(w_in[:, 4096 + s * 1024:4096 + (s + 1) * 1024]),
                     "wza": _wr(w_in[:, 6144:6160]),
                     "wa2": np.ascontiguousarray(wb["gla_w_a2"][j][:, s * 512:(s + 1) * 512]),
                     "ba": _vec(A(gla_b_a)[j][s * 512:(s + 1) * 512]),
                     "gnw": np.ascontiguousarray(A(gla_gn_w)[j][s * 1024:(s + 1) * 1024][None, :]),
                     "mask_incl": iu, "ident": ident}
                in_maps.append(m)
            res = _run(ncm, in_maps)
            mix_out = [np.asarray(r["og"]) for r in res]
            wo = wb["gla_w_o"][j]
        last = (i == 3)
        nc3 = _prog(("s3", last), lambda: build_s3(last))
        wo_l, w1_l, w2_l = _wl(wo), _wl(wb["mlp_w1"][i]), _wl(wb["mlp_w2"][i])
        g_next = _vec(norm_f if last else norm_mix[i + 1])
        in_maps = []
        for c in range(NCORE):
            b, s = c // 2, c % 2
            ts = slice(s * NTC, (s + 1) * NTC)
            yg = np.concatenate([mix_out[2 * b][:, :, ts], mix_out[2 * b + 1][:, :, ts]], axis=1)
            in_maps.append({"yg": np.ascontiguousarray(yg), "h_in": h[c], "wo": wo_l, "w1": w1_l, "w2": w2_l,
                            "g_mlp": _vec(norm_mlp[i]), "g_next": g_next})
        res = _run(nc3, in_maps)
        if not last:
            h = [np.asarray(r["h_out"]) for r in res]
            hn = [np.asarray(r["hn_out"]) for r in res]
        else:
            outs = [np.asarray(r["out"]) for r in res]
    out = np.zeros((NB, SEQ, D_MODEL), f32)
    for b in range(NB):
        full = np.concatenate([_unfm(outs[2 * b]), _unfm(outs[2 * b + 1])], axis=1)
        out[b] = full[:, NMETA:].T
    return out
```
